# Optimizing a Trainium2 kernel written in Bass

```python
import math
import jax, jax.numpy as jnp
from jax import lax
import numpy as np

D_MODEL = 1024
BATCH = 4
SEQ = 8192
DEPTH = 1

CHUNK = 64
A_HEADS = 8
A_DK = 128
A_DV = 128
A_CONV = 4
A_W = A_HEADS * A_DV
B_HEADS = 16
B_DH = 64
B_W = B_HEADS * B_DH
B_PREV_CHUNKS = 8
B_MAX_REL = 256
B_REL_SIZE = CHUNK - 1 + B_MAX_REL + 1
D_FF = 2816
FFN_CONV = 3
N_BRANCHES = 2
IN_SPLITS = (3 * A_W, 4 * A_W, 4 * A_W + A_HEADS, 4 * A_W + 2 * A_HEADS,
             4 * A_W + 2 * A_HEADS + B_W, 4 * A_W + 2 * A_HEADS + 2 * B_W,
             4 * A_W + 2 * A_HEADS + 3 * B_W)
IN_COLS = 4 * A_W + 2 * A_HEADS + 3 * B_W + N_BRANCHES * D_MODEL
DEEPNORM_ALPHA = (2.0 * DEPTH) ** 0.25
DEEPNORM_BETA = (8.0 * DEPTH) ** -0.25
LN_EPS = 1e-5
RMS_EPS = 1e-6
L2_EPS = 1e-6
NEG_INF = -1e30

kernel_name = "hybrid_deltanet_bandattn_convffn_deepnorm_adaln"


def layernorm(x, g, b):
    xf = x.astype(jnp.float32)
    mu = jnp.mean(xf, axis=-1, keepdims=True)
    var = jnp.mean(jnp.square(xf - mu), axis=-1, keepdims=True)
    y = (xf - mu) * lax.rsqrt(var + LN_EPS) * g.astype(jnp.float32) + b.astype(jnp.float32)
    return y.astype(x.dtype)


def causal_dwconv(x, w):
    k_width, ch = w.shape
    return lax.conv_general_dilated(
        x, w[:, None, :].astype(x.dtype), window_strides=(1,), padding=[(k_width - 1, 0)],
        dimension_numbers=("NWC", "WIO", "NWC"), feature_group_count=ch)


def l2norm(x):
    return x * lax.rsqrt(jnp.sum(jnp.square(x), axis=-1, keepdims=True) + L2_EPS)


def chunk_gated_delta_rule(q, k, v, g, beta):
    b_, s_, h_, dk = q.shape
    dv = v.shape[-1]
    n_chunks = s_ // CHUNK

    def to_chunks(t):
        return t.reshape(b_, n_chunks, CHUNK, h_, -1).transpose(0, 1, 3, 2, 4)

    q, k, v = to_chunks(q), to_chunks(k), to_chunks(v)
    g = g.reshape(b_, n_chunks, CHUNK, h_).transpose(0, 1, 3, 2)
    beta = beta.reshape(b_, n_chunks, CHUNK, h_).transpose(0, 1, 3, 2)
    g = jnp.cumsum(g, axis=-1)

    causal = jnp.tril(jnp.ones((CHUNK, CHUNK), dtype=bool))
    strict = jnp.tril(jnp.ones((CHUNK, CHUNK), dtype=bool), k=-1)
    diff = g[..., :, None] - g[..., None, :]
    decay = jnp.where(causal, jnp.exp(jnp.where(causal, diff, 0.0)), 0.0)

    k_beta = k * beta[..., None]
    v_beta = v * beta[..., None]
    a_low = jnp.where(strict, jnp.einsum("bnhid,bnhjd->bnhij", k_beta, k) * decay, 0.0)
    eye = jnp.eye(CHUNK, dtype=jnp.float32)
    rhs = jnp.concatenate([v_beta, k_beta * jnp.exp(g)[..., None]], axis=-1)
    sol = lax.linalg.triangular_solve(a_low + eye, rhs, left_side=True, lower=True,
                                      unit_diagonal=True)
    u = sol[..., :dv]
    w = sol[..., dv:]
    qk = jnp.where(causal, jnp.einsum("bnhid,bnhjd->bnhij", q, k) * decay, 0.0)

    def step(state, inp):
        q_n, k_n, u_n, w_n, qk_n, g_n = inp
        v_new = u_n - jnp.einsum("bhck,bhkv->bhcv", w_n, state)
        o_n = (jnp.einsum("bhck,bhkv->bhcv", q_n * jnp.exp(g_n)[..., None], state)
               + jnp.einsum("bhij,bhjv->bhiv", qk_n, v_new))
        g_last = g_n[..., -1]
        k_dec = k_n * jnp.exp(g_last[..., None] - g_n)[..., None]
        state = state * jnp.exp(g_last)[..., None, None] + jnp.einsum("bhck,bhcv->bhkv", k_dec, v_new)
        return state, o_n

    xs = tuple(jnp.moveaxis(t, 1, 0) for t in (q, k, u, w, qk, g))
    state0 = jnp.zeros((b_, h_, dk, dv), jnp.float32)
    _, o = lax.scan(step, state0, xs)
    return o.transpose(1, 0, 3, 2, 4).reshape(b_, s_, h_, dv)


def gated_deltanet(qkv, z, beta_raw, a_raw, conv_w, a_log, dt_bias, norm_w):
    b_, s_, _ = qkv.shape
    qkv = jax.nn.silu(causal_dwconv(qkv, conv_w))
    q, k, v = jnp.split(qkv.astype(jnp.float32), 3, axis=-1)
    q = l2norm(q.reshape(b_, s_, A_HEADS, A_DK)) * (A_DK ** -0.5)
    k = l2norm(k.reshape(b_, s_, A_HEADS, A_DK))
    v = v.reshape(b_, s_, A_HEADS, A_DV)
    beta = jax.nn.sigmoid(beta_raw.astype(jnp.float32))
    g = -jnp.exp(a_log.astype(jnp.float32)) * jax.nn.softplus(
        a_raw.astype(jnp.float32) + dt_bias.astype(jnp.float32))
    o = chunk_gated_delta_rule(q, k, v, g, beta)
    o = o * lax.rsqrt(jnp.mean(jnp.square(o), axis=-1, keepdims=True) + RMS_EPS)
    o = o * norm_w.astype(jnp.float32) * jax.nn.silu(z.astype(jnp.float32).reshape(b_, s_, A_HEADS, A_DV))
    return o.reshape(b_, s_, A_W).astype(qkv.dtype)


def chunk_band_attention(q, k, v, rel_bias):
    b_, s_, h_, dh = q.shape
    n_chunks = s_ // CHUNK
    pad = B_PREV_CHUNKS * CHUNK
    band = (B_PREV_CHUNKS + 1) * CHUNK
    k_pad = jnp.pad(k, ((0, 0), (pad, 0), (0, 0), (0, 0)))
    v_pad = jnp.pad(v, ((0, 0), (pad, 0), (0, 0), (0, 0)))
    qi = np.arange(CHUNK)[:, None]
    kj = np.arange(band)[None, :]
    dist = pad + qi - kj
    idx = np.clip(dist, -(CHUNK - 1), B_MAX_REL) + (CHUNK - 1)
    bias = rel_bias.astype(jnp.float32)[:, idx]
    key_chunk = jnp.arange(band) // CHUNK
    scale = dh ** -0.5

    def one_chunk(n):
        q_n = lax.dynamic_slice_in_dim(q, n * CHUNK, CHUNK, axis=1)
        k_n = lax.dynamic_slice_in_dim(k_pad, n * CHUNK, band, axis=1)
        v_n = lax.dynamic_slice_in_dim(v_pad, n * CHUNK, band, axis=1)
        valid = (n - B_PREV_CHUNKS + key_chunk) >= 0
        s = jnp.einsum("bqhd,bkhd->bhqk", q_n, k_n).astype(jnp.float32) * scale + bias
        s = jnp.where(valid, s, NEG_INF)
        p = jax.nn.softmax(s, axis=-1).astype(v.dtype)
        return jnp.einsum("bhqk,bkhd->bqhd", p, v_n)

    out = lax.map(one_chunk, jnp.arange(n_chunks))
    return out.transpose(1, 0, 2, 3, 4).reshape(b_, s_, h_ * dh)


def setup_inputs(seed: int = 0) -> dict:
    key = jax.random.key(seed)
    ks = jax.random.split(key, 24)

    def nrm(k, shape, scale):
        return jax.random.normal(k, shape, jnp.float32) * scale

    L = DEPTH
    x = nrm(ks[0], (BATCH, SEQ, D_MODEL), 1.0)
    c = nrm(ks[1], (BATCH, D_MODEL), 1.0)
    w_ada = nrm(ks[2], (L, D_MODEL, 6 * D_MODEL), D_MODEL ** -0.5)
    b_ada = nrm(ks[3], (L, 6 * D_MODEL), 0.02)
    w_in = nrm(ks[4], (L, D_MODEL, IN_COLS), D_MODEL ** -0.5)
    b_gate = nrm(ks[5], (L, N_BRANCHES * D_MODEL), 0.1)
    conv_a = nrm(ks[6], (L, A_CONV, 3 * A_W), A_CONV ** -0.5)
    a_log = jnp.log(jax.random.uniform(ks[7], (L, A_HEADS), jnp.float32, minval=1.0, maxval=16.0))
    dt = jnp.exp(jax.random.uniform(ks[8], (L, A_HEADS), jnp.float32,
                                    minval=math.log(1e-3), maxval=math.log(1e-1)))
    dt_bias = dt + jnp.log(-jnp.expm1(-dt))
    norm_a = 1.0 + nrm(ks[9], (L, A_DV), 0.05)
    rel_bias = nrm(ks[10], (L, B_HEADS, B_REL_SIZE), 0.2)
    w_branch_a = nrm(ks[11], (L, A_W, D_MODEL), A_W ** -0.5)
    w_branch_b = nrm(ks[12], (L, B_W, D_MODEL), B_W ** -0.5)
    w_o = nrm(ks[13], (L, D_MODEL, D_MODEL), DEEPNORM_BETA * D_MODEL ** -0.5)
    ln1_g = 1.0 + nrm(ks[14], (L, D_MODEL), 0.05)
    ln1_b = nrm(ks[15], (L, D_MODEL), 0.02)
    w_up = nrm(ks[16], (L, D_MODEL, 2 * D_FF), D_MODEL ** -0.5)
    conv_ffn = nrm(ks[17], (L, FFN_CONV, 2 * D_FF), FFN_CONV ** -0.5)
    b_conv_ffn = nrm(ks[18], (L, 2 * D_FF), 0.02)
    w_down = nrm(ks[19], (L, D_FF, D_MODEL), DEEPNORM_BETA * D_FF ** -0.5)
    ln2_g = 1.0 + nrm(ks[20], (L, D_MODEL), 0.05)
    ln2_b = nrm(ks[21], (L, D_MODEL), 0.02)
    return {"x": x, "c": c, "w_ada": w_ada, "b_ada": b_ada, "w_in": w_in, "b_gate": b_gate,
            "conv_a": conv_a, "a_log": a_log, "dt_bias": dt_bias, "norm_a": norm_a,
            "rel_bias": rel_bias, "w_branch_a": w_branch_a, "w_branch_b": w_branch_b, "w_o": w_o,
            "ln1_g": ln1_g, "ln1_b": ln1_b, "w_up": w_up, "conv_ffn": conv_ffn,
            "b_conv_ffn": b_conv_ffn, "w_down": w_down, "ln2_g": ln2_g, "ln2_b": ln2_b}


def reference(x, c, w_ada, b_ada, w_in, b_gate, conv_a, a_log, dt_bias, norm_a, rel_bias,
              w_branch_a, w_branch_b, w_o, ln1_g, ln1_b, w_up, conv_ffn, b_conv_ffn, w_down,
              ln2_g, ln2_b):
    b_, s_, d_ = x.shape
    c_act = jax.nn.silu(c)
    for l in range(DEPTH):
        mod = (c_act @ w_ada[l] + b_ada[l])[:, None, :]
        shift_t, scale_t, gate_t, shift_f, scale_f, gate_f = jnp.split(mod, 6, axis=-1)

        h = x * (1.0 + scale_t) + shift_t
        proj = h @ w_in[l]
        qkv_a, z_a, beta_raw, a_raw, q_b, k_b, v_b, gates = jnp.split(proj, IN_SPLITS, axis=-1)

        o_a = gated_deltanet(qkv_a, z_a, beta_raw, a_raw, conv_a[l], a_log[l], dt_bias[l], norm_a[l])
        o_b = chunk_band_attention(q_b.reshape(b_, s_, B_HEADS, B_DH),
                                   k_b.reshape(b_, s_, B_HEADS, B_DH),
                                   v_b.reshape(b_, s_, B_HEADS, B_DH), rel_bias[l])

        gate_a, gate_b = jnp.split(jax.nn.sigmoid(gates + b_gate[l]), N_BRANCHES, axis=-1)
        merged = gate_a * (o_a @ w_branch_a[l]) + gate_b * (o_b @ w_branch_b[l])
        mix = merged @ w_o[l]
        x = layernorm(DEEPNORM_ALPHA * x + gate_t * mix, ln1_g[l], ln1_b[l])

        h = x * (1.0 + scale_f) + shift_f
        u = causal_dwconv(h @ w_up[l], conv_ffn[l]) + b_conv_ffn[l]
        u_gate, u_val = jnp.split(u, 2, axis=-1)
        ffn = (jax.nn.silu(u_gate) * u_val) @ w_down[l]
        x = layernorm(DEEPNORM_ALPHA * x + gate_f * ffn, ln2_g[l], ln2_b[l])
    return x
```

```python
import contextlib
import math
import numpy as np
import concourse.bass as bass
import concourse.mybir as mybir
from concourse.bass_utils import run_bass_kernel_spmd

F32 = mybir.dt.float32
BF = mybir.dt.bfloat16
AF = mybir.ActivationFunctionType
ALU = mybir.AluOpType

ALPHA = 2.0 ** 0.25
W = 512
NEG = -30000.0
ENGS = ("pe", "act", "dve", "pool", "sp")


class Op:
    __slots__ = ("eng", "fn", "deps", "needs_inc", "tok", "dma", "epoch", "nd")

    def __init__(self, eng, fn, dma=None):
        self.eng = eng
        self.fn = fn
        self.deps = []
        self.needs_inc = False
        self.tok = None
        self.dma = dma
        self.nd = 1


class Prog:
    def __init__(self, nc):
        self.nc = nc
        self.ops = []
        self.last_w = {}
        self.readers = {}
        self.epoch = 0

    def next_epoch(self):
        self.epoch += 1

    PSUM_KEYS = frozenset(["PA0", "PA1", "PG0", "PG1", "PX", "PY", "PZ", "PV"])

    def _add(self, op, reads, writes):
        op.epoch = self.epoch
        writes = list(writes) + [k for k in reads if k in self.PSUM_KEYS and k not in writes]
        deps = {}
        for k in reads:
            w = self.last_w.get(k)
            if w is not None:
                deps[id(w)] = (w, "raw")
        for k in writes:
            w = self.last_w.get(k)
            if w is not None and id(w) not in deps:
                deps[id(w)] = (w, "waw")
            for r in self.readers.get(k, ()):
                if id(r) not in deps:
                    deps[id(r)] = (r, "war")
        for (d, kind) in deps.values():
            if d is op:
                continue
            same = (d.eng == op.eng) and d.dma is None and op.dma is None
            if same and op.eng == "pe":
                continue
            op.deps.append(d)
            d.needs_inc = True
        for k in writes:
            self.last_w[k] = op
            self.readers[k] = []
        for k in reads:
            if k not in writes:
                self.readers.setdefault(k, []).append(op)
        self.ops.append(op)
        return op

    def op(self, eng, fn, reads=(), writes=()):
        return self._add(Op(eng, fn), list(reads), list(writes))

    def dma(self, eng, pairs, reads=(), writes=(), sem=None):
        def fn(e, pairs=pairs):
            return [e.dma_start(out=o, in_=i) for (o, i) in pairs]
        op = Op(eng, fn, dma=sem)
        op.nd = len(pairs)
        return self._add(op, list(reads), list(writes))

    def emit(self):
        nc = self.nc
        cnt = {}
        dma_cnt = {}
        sem_names = []
        seen = set()
        for op in self.ops:
            if op.dma is not None:
                key = ("dma", op.dma)
                dma_cnt[key] = dma_cnt.get(key, 0) + 16 * op.nd
                op.tok = (key, dma_cnt[key])
            elif op.needs_inc:
                key = (op.eng, op.epoch)
                cnt[key] = cnt.get(key, 0) + 1
                op.tok = (key, cnt[key])
            else:
                continue
            if key not in seen:
                seen.add(key)
                sem_names.append(key)
        per_eng = {e: [] for e in ENGS}
        for op in self.ops:
            per_eng[op.eng].append(op)
        self.n_sems = len(sem_names)
        with contextlib.ExitStack() as st:
            sems = {}
            for i, key in enumerate(sem_names):
                sems[key] = st.enter_context(nc.semaphore("s%d" % i))
            block = st.enter_context(nc.Block())

            def run(engine, ops):
                waited = {}
                for op in ops:
                    for d in op.deps:
                        key, val = d.tok
                        if waited.get(key, 0) >= val:
                            continue
                        engine.wait_ge(sems[key], val)
                        waited[key] = val
                    if op.fn is None:
                        continue
                    r = op.fn(engine)
                    if op.dma is not None:
                        for ins in r:
                            ins.then_inc(sems[op.tok[0]], 16)
                    elif op.needs_inc:
                        r.then_inc(sems[op.tok[0]], 1)

            @block.tensor
            def _(e):
                run(e, per_eng["pe"])

            @block.scalar
            def _(e):
                run(e, per_eng["act"])

            @block.vector
            def _(e):
                run(e, per_eng["dve"])

            @block.gpsimd
            def _(e):
                run(e, per_eng["pool"])

            @block.sync
            def _(e):
                run(e, per_eng["sp"])


IN_SHAPES = {
    "ccol": [128, 8], "flag": [128, 1], "cmask": [128, 512],
    "w_ada": [1024, 6144], "b_ada": [128, 48], "w_in": [1024, 9232], "b_gate": [128, 16],
    "conv_a": [128, 96], "alog_tm": [128, 32], "dtb_tm": [128, 32], "alog_fm": [8, 1], "dtb_fm": [8, 1],
    "norm_a": [128, 1], "battn": [128, 16 * 384], "bconst": [128, 16], "amask": [128, 384],
    "maskd1": [128, 64], "w_ba": [1024, 1024], "w_bb": [1024, 1024], "w_o": [1024, 1024],
    "lnp": [128, 32], "w_up": [1024, 5632], "conv_f": [128, 132], "bconv": [128, 44],
    "w_down": [2816, 1024], "ident_f": [128, 128], "tri2": [128, 128], "sgt": [128, 128], "tri_blk": [128, 128], "onesh": [128, 256],
    "negm2": [128, 128], "scanmask": [8, 512], "sel": [8, 1024], "masks4": [128, 1024],
}


def host_consts():
    c = {}
    c["ident_f"] = np.eye(128, dtype=np.float32)
    p_ = np.arange(128)
    same = (p_[:, None] // 64) == (p_[None, :] // 64)
    k = (p_ % 64)[:, None]
    i = (p_ % 64)[None, :]
    c["tri_blk"] = ((k <= i) & same).astype(np.float32)
    c["sgt"] = ((k > i) & same).astype(np.float32)
    i64 = np.arange(64)[None, :]
    tri = (k <= i64).astype(np.float32)
    c["tri2"] = np.concatenate([tri, tri], axis=1)
    strict = np.where(k < i64, 0.0, NEG).astype(np.float32)
    incl = np.where(k <= i64, 0.0, NEG).astype(np.float32)
    c["negm2"] = np.concatenate([strict, incl], axis=1)
    oh = np.zeros((128, 2, 128), np.float32)
    oh[:64, 0, :] = 1.0
    oh[64:, 1, :] = 1.0
    c["onesh"] = oh.reshape(128, 256)
    sm = np.ones((8, 512), np.float32)
    sm[:, ::64] = 0.0
    c["scanmask"] = sm
    sel = np.zeros((8, 8, 128), np.float32)
    for h in range(8):
        sel[h, h, :] = 1.0
    c["sel"] = sel.reshape(8, 1024)
    r_ = (np.arange(128) % 64)[:, None]
    c_ = np.arange(64)[None, :]
    mks = [(r_ // 8 == c_ // 8)]
    for bsz in (8, 16, 32):
        mks.append((r_ // (2 * bsz) == c_ // (2 * bsz)) & (r_ // bsz != c_ // bsz))
    mk = np.stack([np.broadcast_to(m_[:, None, :], (128, 4, 64)) for m_ in mks], axis=1).astype(np.float32)
    c["masks4"] = np.ascontiguousarray(mk.reshape(128, 1024))
    am = np.zeros((128, 6, 64), np.float32)
    am[64:, 0, :] = NEG
    c["amask"] = am.reshape(128, 384)
    md = np.zeros((128, 64), np.float32)
    md[:64, :] = NEG
    c["maskd1"] = md
    return c


def fm_vec(v, n):
    return np.ascontiguousarray(np.asarray(v, np.float32).reshape(n, 128).T)


def host_shared(inp):
    s = dict(host_consts())
    s["w_ada"] = np.ascontiguousarray(inp["w_ada"][0])
    s["b_ada"] = fm_vec(inp["b_ada"][0], 48)
    s["w_in"] = np.ascontiguousarray(inp["w_in"][0])
    s["b_gate"] = fm_vec(inp["b_gate"][0], 16)
    ca = np.asarray(inp["conv_a"][0], np.float32)
    s["conv_a"] = np.ascontiguousarray(ca.reshape(4, 24, 128).transpose(2, 1, 0).reshape(128, 96))
    al = np.asarray(inp["a_log"][0], np.float32)
    db = np.asarray(inp["dt_bias"][0], np.float32)
    s["alog_tm"] = np.ascontiguousarray(np.broadcast_to(np.tile(al, 4)[None, :], (128, 32)))
    s["dtb_tm"] = np.ascontiguousarray(np.broadcast_to(np.tile(db, 4)[None, :], (128, 32)))
    s["alog_fm"] = np.ascontiguousarray(al.reshape(8, 1))
    s["dtb_fm"] = np.ascontiguousarray(db.reshape(8, 1))
    s["norm_a"] = np.ascontiguousarray(np.asarray(inp["norm_a"][0], np.float32).reshape(128, 1))
    rb = np.asarray(inp["rel_bias"][0], np.float32)
    kj = np.arange(128)[:, None, None]
    dl = np.arange(6)[None, :, None]
    qi = np.arange(64)[None, None, :]
    dist = dl * 64 + qi - kj
    idx = np.clip(dist, -63, 256) + 63
    s["battn"] = np.ascontiguousarray(rb[:, idx].transpose(1, 0, 2, 3).reshape(128, 16 * 384))
    s["bconst"] = np.ascontiguousarray(np.broadcast_to(rb[:, 319][None, :], (128, 16)))
    s["w_ba"] = np.ascontiguousarray(inp["w_branch_a"][0])
    s["w_bb"] = np.ascontiguousarray(inp["w_branch_b"][0])
    s["w_o"] = np.ascontiguousarray(inp["w_o"][0])
    s["lnp"] = np.concatenate([fm_vec(inp["ln1_g"][0], 8), fm_vec(inp["ln1_b"][0], 8),
                               fm_vec(inp["ln2_g"][0], 8), fm_vec(inp["ln2_b"][0], 8)], axis=1)
    s["w_up"] = np.ascontiguousarray(inp["w_up"][0])
    cf = np.asarray(inp["conv_ffn"][0], np.float32)
    s["conv_f"] = np.ascontiguousarray(cf.reshape(3, 44, 128).transpose(2, 1, 0).reshape(128, 132))
    s["bconv"] = fm_vec(inp["b_conv_ffn"][0], 44)
    s["w_down"] = np.ascontiguousarray(inp["w_down"][0])
    return s


def build(modes, out_start, first_main, dbg_names=()):
    NT = len(modes)
    n_out = NT - out_start
    nc = bass.Bass("TRN2", target_bir_lowering=False, dynamic_dma_scratch_size=4096)
    D = {}
    D["xT"] = nc.dram_tensor("xT", [1024, NT * W], F32, kind="ExternalInput").ap()
    for name, shp in IN_SHAPES.items():
        D[name] = nc.dram_tensor(name, list(shp), F32, kind="ExternalInput").ap()
    outT = nc.dram_tensor("outT", [1024, n_out * W], F32, kind="ExternalOutput").ap()
    dbg_out = {}

    G = {}
    def addg(name, src, nk, c0, ncols):
        G[name] = (src, nk, c0, ncols)
    for i in range(2):
        addg("qa%d" % i, "w_in", 8, 0 + 512 * i, 512)
        addg("ka%d" % i, "w_in", 8, 1024 + 512 * i, 512)
        addg("va%d" % i, "w_in", 8, 2048 + 512 * i, 512)
        addg("za%d" % i, "w_in", 8, 3072 + 512 * i, 512)
        addg("qb%d" % i, "w_in", 8, 4112 + 512 * i, 512)
        addg("kb%d" % i, "w_in", 8, 5136 + 512 * i, 512)
        addg("vb%d" % i, "w_in", 8, 6160 + 512 * i, 512)
        addg("wa%d" % i, "w_ba", 8, 512 * i, 512)
        addg("wb%d" % i, "w_bb", 8, 512 * i, 512)
        addg("wo%d" % i, "w_o", 8, 512 * i, 512)
    addg("ba", "w_in", 8, 4096, 16)
    for i in range(4):
        addg("g%d" % i, "w_in", 8, 7184 + 512 * i, 512)
    for i in range(11):
        addg("up%d" % i, "w_up", 8, 512 * i, 512)
    for i in range(8):
        addg("dn%d" % i, "w_down", 22, 128 * i, 128)
    SCR = {}
    for name, (src, nk, c0, ncols) in G.items():
        SCR[name] = nc.dram_tensor("scr_" + name, [128, nk, ncols], BF, kind="Internal").ap()

    P = Prog(nc)
    st = contextlib.ExitStack()
    with st:
        def sb(name, shape, dt=F32):
            return st.enter_context(nc.sbuf_tensor(name, list(shape), dt))

        def pst(name, shape, dt=F32):
            return st.enter_context(nc.psum_tensor(name, list(shape), dt))

        C = {}
        small = ["ccol", "flag", "b_ada", "b_gate", "conv_a", "alog_tm", "dtb_tm", "alog_fm", "dtb_fm",
                 "norm_a", "bconst", "lnp", "conv_f", "bconv", "ident_f", "tri2", "sgt", "negm2",
                 "scanmask", "sel", "masks4", "tri_blk", "onesh"]
        for name in small:
            C[name] = sb("c_" + name, IN_SHAPES[name])
        ident_bf = sb("ident_bf", [128, 128], BF)
        nident_bf = sb("nident_bf", [128, 128], BF)
        ones_bf = sb("ones_bf", [128, 128], BF)
        onesD_bf = sb("onesD_bf", [128, 128], BF)
        cmask_bf = sb("cmask_bf", [128, 512], BF)
        maskd1_bf = sb("maskd1_bf", [128, 64], BF)
        negm2_bf = sb("negm2_bf", [128, 128], BF)
        biasbf = sb("biasbf", [128, 16, 384], BF)
        modv = sb("modv", [128, 48])
        cact = sb("cact", [128, 8])
        A1 = sb("A1", [128, 8]); G1a = sb("G1a", [128, 8]); B1a = sb("B1a", [128, 8])
        G1h = sb("G1h", [128, 8]); B1h = sb("B1h", [128, 8]); sf1 = sb("sf1", [128, 8])
        nA_tm = sb("nA_tm", [128, 32]); nA_fm = sb("nA_fm", [8, 1])
        Sf = sb("Sf", [128, 8, 128]); Sb = sb("Sb", [128, 8, 128], BF)
        Kbuf = sb("Kbuf", [128, 8, 2, 512], BF)
        Vbuf = sb("Vbuf", [128, 2, 4, 1024], BF)
        tails_a = sb("tails_a", [128, 24, 3]); tails_f = sb("tails_f", [128, 44, 2])
        NSLOT = 2
        wbuf = sb("wbuf", [128, NSLOT, 4096], BF)
        xa = sb("xa", [128, 8, 512]); x1 = sb("x1", [128, 8, 512])
        hb = sb("hb", [128, 8, 512], BF)
        btm = sb("btm", [128, 32]); grtm = sb("grtm", [128, 32]); bge = sb("bge", [128, 32])
        kd = sb("kd", [128, 32]); egl = sb("egl", [128, 2, 32]); tmt = sb("tmt", [128, 32])
        bfm = sb("bfm", [8, 512]); grfm = sb("grfm", [8, 512]); gcf = sb("gcf", [8, 512])
        kbg = sb("kbg", [128, 4, 128], BF); kdt = sb("kdt", [128, 4, 128], BF); vbt = sb("vbt", [128, 4, 128], BF)
        sgtg = sb("sgtg", [128, 4, 128]); M2 = sb("M2", [128, 512]); NQ = sb("NQ", [128, 4, 128], BF)
        Asb = sb("Asb", [128, 4, 64], BF); Lb = sb("Lb", [128, 4, 256], BF); YZ = sb("YZ", [128, 4, 128], BF)
        nM = sb("nM", [128, 4, 128], BF); UM = sb("UM", [128, 4, 4, 64], BF); AM = sb("AM", [128, 4, 4, 64], BF)
        nwT = sb("nwT", [128, 4, 64], BF); vn = sb("vn", [128, 4, 128], BF)
        cst = sb("cst", [128, 2, 516]); acc = sb("acc", [128, 2, 512])
        sqb = sb("sqb", [128, 2, 512], BF); rnb = sb("rnb", [128, 512]); rn2 = sb("rn2", [128, 512])
        ybt = sb("ybt", [128, 2, 512], BF)
        dummy = sb("dmy_bar", [128, 8])
        rcb = ybt[:, :, :].rearrange("p a b -> p (a b)").bitcast(F32)
        TA = sb("TA", [128, 16384], BF)
        TB = sb("TB", [128, 8192], BF)

        def view(ar, off, shape, dt):
            n = int(np.prod(shape[1:]))
            nb = n * (4 if dt == F32 else 2)
            a = ar[:, off // 2:(off + nb) // 2]
            if dt == F32:
                a = a.bitcast(F32)
            if len(shape) == 3:
                a = a.rearrange("p (a b) -> p a b", a=shape[1])
            elif len(shape) == 4:
                a = a.rearrange("p (a b c) -> p a b c", a=shape[1], b=shape[2])
            return a
        KB = 1024
        kf = view(TA, 0, [128, 4, 512], BF)
        kq = view(TA, 4 * KB, [128, 4, 2, 512], BF)
        qe = view(TA, 12 * KB, [128, 4, 512], BF)
        vf = view(TA, 16 * KB, [128, 4, 512], BF)
        oraw = view(TA, 20 * KB, [128, 4, 512], F32)
        zs = view(TA, 28 * KB, [128, 4, 512], BF)
        x1b = x1[:, :, :].rearrange("p a b -> p (a b)").bitcast(BF)
        qb = x1b[:, 0:4096].rearrange("p (a b) -> p a b", a=8)
        PTb = x1b[:, 4096:4096 + 2560]
        gsig = view(TA, 0, [128, 16, 512], BF)
        ta_ = view(TA, 16 * KB, [128, 4, 512], F32)
        merged = view(TA, 24 * KB, [128, 8, 512], BF)
        sg = view(TA, 0, [128, 22, 512], BF)
        VW = {}
        VW["F"] = dict(kf=kf, kq=kq, qe=qe, vf=vf, oraw=oraw, k_kf="kf", k_kq="kq", k_qe="qe", k_vf="vf", k_oraw="oraw")
        for h_ in range(2):
            b_ = h_ * 16 * KB
            VW["S%d" % h_] = dict(kf=view(TA, b_, [128, 4, 512], BF), kq=view(TA, b_ + 4 * KB, [128, 4, 2, 512], BF),
                                  vf=view(TA, b_ + 12 * KB, [128, 4, 512], BF), qe=None, oraw=None,
                                  k_kf="kfS%d" % h_, k_kq="kqS%d" % h_, k_vf="vfS%d" % h_, k_qe="qeS", k_oraw="orawS")
        SMODE_KEYS = ["kfS0", "kqS0", "vfS0", "kfS1", "kqS1", "vfS1"]
        setA = dict(kbg=kbg, kdt=kdt, vbt=vbt, sgtg=sgtg, M2=M2, NQ=NQ, Asb=Asb, Lb=Lb, YZ=YZ, nM=nM, UM=UM, AM=AM,
                    nwT=nwT, vn=vn, tag="A")
        o_ = [0]

        def tbv(shape, dt):
            n = int(np.prod(shape[1:])) * (4 if dt == F32 else 2)
            v_ = view(TB, o_[0], shape, dt)
            o_[0] += n
            return v_
        setB = dict(kbg=tbv([128, 4, 128], BF), kdt=tbv([128, 4, 128], BF), vbt=tbv([128, 4, 128], BF),
                    sgtg=tbv([128, 4, 128], F32), NQ=tbv([128, 4, 128], BF), Asb=tbv([128, 4, 64], BF),
                    Lb=tbv([128, 4, 256], BF), YZ=tbv([128, 4, 128], BF), nM=tbv([128, 4, 128], BF),
                    UM=tbv([128, 4, 4, 64], BF), AM=tbv([128, 4, 4, 64], BF), nwT=tbv([128, 4, 64], BF),
                    vn=tbv([128, 4, 128], BF), M2=rn2, tag="B")
        assert o_[0] <= 16 * KB, o_[0]
        SETB_KEYS = [n_ + "B" for n_ in ("kbg", "kdt", "vbt", "sgtg", "NQ", "Asb", "L0", "L1", "YZ0", "YZ1", "nM0", "nM1",
                                        "UM", "AM", "nwT", "vn", "M2")]
        o_a = view(TB, 0, [128, 8, 512], BF)
        o_b = view(TB, 8 * KB, [128, 8, 512], BF)
        PA = [pst("PA0", [128, 512]), pst("PA1", [128, 512])]
        PG = [pst("PGa", [128, 512]), pst("PGb", [128, 512])]
        PX = pst("PX", [128, 512]); PY = pst("PY", [128, 512]); PZ = pst("PZ", [128, 512]); PV = pst("PV", [128, 512])
        PXb = PX[:, :].bitcast(BF)

        def mm(out, lhsT, rhs, start, stop, r, w):
            P.op("pe", lambda e: e.matmul(out, lhsT=lhsT, rhs=rhs, start=start, stop=stop), r, w)

        def tr(out, in_, ident, r, w):
            P.op("pe", lambda e: e.transpose(out, in_, ident), r, w)

        def act(out, in_, func, r, w, bias=None, scale=None):
            kw = {}
            if bias is not None:
                kw["bias"] = bias
            if scale is not None:
                kw["scale"] = scale
            P.op("act", lambda e: e.activation(out=out, in_=in_, func=func, **kw), r, w)

        def tt(eng, out, in0, in1, op, r, w):
            P.op(eng, lambda e: e.tensor_tensor(out=out, in0=in0, in1=in1, op=op), r, w)

        def ts(eng, out, in0, s1, s2, op0, op1, r, w):
            if op1 is None:
                P.op(eng, lambda e: e.tensor_scalar(out=out, in0=in0, scalar1=s1, scalar2=None, op0=op0), r, w)
            else:
                P.op(eng, lambda e: e.tensor_scalar(out=out, in0=in0, scalar1=s1, scalar2=s2, op0=op0, op1=op1), r, w)

        def stt(out, in0, scalar, in1, op0, op1, r, w):
            P.op("dve", lambda e: e.scalar_tensor_tensor(out=out, in0=in0, scalar=scalar, in1=in1, op0=op0, op1=op1), r, w)

        def cp(eng, out, in_, r, w):
            if eng == "act":
                P.op("act", lambda e: e.copy(out=out, in_=in_), r, w)
            else:
                P.op(eng, lambda e: e.tensor_copy(out=out, in_=in_), r, w)

        def mset(eng, ap, val, w):
            P.op(eng, lambda e: e.memset(ap, val), (), w)

        def barrier(old, new):
            P.op("pool", lambda e: e.memset(dummy[:, 0:1], 0.0), list(old), list(new) + list(old))

        def dbg(name, ap, shape, r):
            if name not in dbg_names:
                return
            d = nc.dram_tensor("dbg_" + name, list(shape), ap.dtype, kind="ExternalOutput").ap()
            dbg_out[name] = d
            P.dma("sp", [(d, ap)], reads=r, writes=["dbg_" + name], sem="dbg_" + name)

        rr = [0]
        def evac_eng():
            rr[0] ^= 1
            return "act" if rr[0] else "dve"

        for name in small:
            P.dma("sp", [(C[name][:, :], D[name])], reads=(), writes=["c_" + name], sem="ld_" + name)
        mset("pool", Sf[:, :, :], 0.0, ["Sf0", "Sf1"]); mset("pool", Sb[:, :, :], 0.0, ["Sb0", "Sb1"])
        mset("pool", Kbuf[:, :, :, :], 0.0, ["Kbuf0", "Kbuf1"]); mset("pool", Vbuf[:, :, :, :], 0.0, ["Vbuf0", "Vbuf1"])
        mset("pool", tails_a[:, :, :], 0.0, ["tails_a%d" % i for i in range(24)])
        mset("pool", tails_f[:, :, :], 0.0, ["tails_f%d" % i for i in range(44)])
        mset("pool", TA[:, :], 0.0, ["kf", "kq", "qe", "vf", "oraw", "zs"])
        mset("pool", ones_bf[:, :], 1.0, ["ones_bf"]); mset("pool", onesD_bf[:, :], 1.0 / 1024.0, ["onesD_bf"])
        cp("dve", ident_bf[:, :], C["ident_f"][:, :], ["c_ident_f"], ["ident_bf"])
        cp("dve", negm2_bf[:, :], C["negm2"][:, :], ["c_negm2"], ["negm2_bf"])
        ts("dve", nident_bf[:, :], C["ident_f"][:, :], -1.0, None, ALU.mult, None, ["c_ident_f"], ["nident_bf"])
        P.dma("sp", [(acc[:, 0, :], D["cmask"])], (), ["acc0"], sem="ld_cmask")
        cp("dve", cmask_bf[:, :], acc[:, 0, :], ["acc0"], ["cmask_bf"])
        P.dma("sp", [(acc[:, 1, 0:64], D["maskd1"])], (), ["acc1"], sem="ld_maskd1")
        cp("dve", maskd1_bf[:, :], acc[:, 1, 0:64], ["acc1"], ["maskd1_bf"])
        P.dma("sp", [(rnb[:, 0:384], D["amask"])], (), ["rnb"], sem="ld_amask")
        for h in range(16):
            k_ = "acc%d" % (h % 2)
            P.dma("sp", [(acc[:, h % 2, 0:384], D["battn"][:, h * 384:(h + 1) * 384])], (), [k_], sem="ld_battn%d" % (h % 2))
            stt(biasbf[:, h, :], acc[:, h % 2, 0:384], C["bconst"][:, h:h + 1], rnb[:, 0:384], ALU.subtract, ALU.add,
                [k_, "rnb", "c_bconst"], ["biasbf"])
        act(nA_tm[:, :], C["alog_tm"][:, :], AF.Exp, ["c_alog_tm"], ["nA_tm"])
        act(nA_fm[:, :], C["alog_fm"][:, :], AF.Exp, ["c_alog_fm"], ["nA_fm"])

        order = ["ba", "ka0", "va0", "qa0", "za0", "ka1", "va1", "qa1", "za1", "kb0", "kb1", "vb0", "vb1",
                 "qb0", "qb1", "g0", "g1", "g2", "g3", "wa0", "wb0", "wa1", "wb1", "wo0", "wo1"]
        order += ["up%d" % i for i in range(11)] + ["dn%d" % i for i in range(8)]
        for name in order:
            src, nk, c0, ncols = G[name]
            sv = D[src].rearrange("(k p) n -> p k n", p=128)[:, :, c0:c0 + ncols]
            P.dma("pool", [(SCR[name], sv)], reads=["castchain"], writes=["scr_" + name, "castchain"], sem="cast")

        act(cact[:, :], C["ccol"][:, :], AF.Silu, ["c_ccol"], ["cact"])
        wst = view(TA, 0, [128, 8, 768], F32)
        for pc in range(8):
            P.dma("sp", [(wst, D["w_ada"].rearrange("(k p) n -> p k n", p=128)[:, :, pc * 768:(pc + 1) * 768])],
                  (), ["kf", "kq", "qe", "vf", "oraw"], sem="ld_wada")
            for jj in range(6):
                j = pc * 6 + jj
                for kc in range(8):
                    mm(PV[:, j:j + 1], wst[:, kc, jj * 128:(jj + 1) * 128], cact[:, kc:kc + 1], kc == 0, kc == 7,
                       ["kf", "cact"], ["PV"])
        tt("dve", modv[:, :], PV[:, 0:48], C["b_ada"][:, :], ALU.add, ["PV", "c_b_ada"], ["modv"])
        sh_t, sc_t, g_t = modv[:, 0:8], modv[:, 8:16], modv[:, 16:24]
        sh_f, sc_f, g_f = modv[:, 24:32], modv[:, 32:40], modv[:, 40:48]
        lnp = C["lnp"]
        ts("dve", A1[:, :], sc_t, 1.0, None, ALU.add, None, ["modv"], ["A1"])
        ts("dve", sf1[:, :], sc_f, 1.0, None, ALU.add, None, ["modv"], ["sf1"])
        ts("dve", G1a[:, :], lnp[:, 0:8], ALPHA, None, ALU.mult, None, ["c_lnp"], ["G1a"])
        ts("dve", B1a[:, :], lnp[:, 8:16], ALPHA, None, ALU.mult, None, ["c_lnp"], ["B1a"])
        tt("dve", G1h[:, :], lnp[:, 0:8], sf1[:, :], ALU.mult, ["c_lnp", "sf1"], ["G1h"])
        tt("dve", B1h[:, :], lnp[:, 8:16], sf1[:, :], ALU.mult, ["c_lnp", "sf1"], ["B1h"])
        tt("dve", B1h[:, :], B1h[:, :], sh_f, ALU.add, ["B1h", "modv"], ["B1h"])
        dbg("modv", modv[:, :], [128, 48], ["modv"])
        mset("pool", TA[:, :], 0.0, ["kf", "kq", "qe", "vf", "oraw", "zs"])

        slot_ctr = [0]

        def wload(name):
            s = slot_ctr[0] % NSLOT
            slot_ctr[0] += 1
            src, nk, c0, ncols = G[name]
            dst = wbuf[:, s, 0:nk * ncols].rearrange("p (k n) -> p k n", k=nk)
            P.dma("sp", [(dst, SCR[name])], reads=["scr_" + name], writes=["wslot%d" % s], sem="wslot%d" % s)
            return (dst, "wslot%d" % s)

        def load_x(t):
            src = D["xT"].rearrange("(k p) n -> p k n", p=128)[:, :, t * W:(t + 1) * W]
            P.dma("sp", [(xa[:, :, :], src)], (), ["xa%d" % k_ for k_ in range(8)], sem="ld_x")

        def make_h(full):
            for kc in range(8):
                if kc % 2 == 0:
                    act(hb[:, kc, :], xa[:, kc, :], AF.Identity, ["xa%d" % kc, "A1", "modv"], ["hb%d" % kc],
                        bias=sh_t[:, kc:kc + 1], scale=A1[:, kc:kc + 1])
                else:
                    ts("dve", hb[:, kc, :], xa[:, kc, :], A1[:, kc:kc + 1], sh_t[:, kc:kc + 1], ALU.mult, ALU.add,
                       ["xa%d" % kc, "A1", "modv"], ["hb%d" % kc])
            if full:
                for kc in range(8):
                    ts("pool", xa[:, kc, :], xa[:, kc, :], ALPHA, None, ALU.mult, None, ["xa%d" % kc], ["xa%d" % kc])

        pa_ctr = [0]
        PJ = [PA[0], PA[1], PG[0], PG[1]]
        PJK = ["PA0", "PA1", "PG0", "PG1"]

        def proj_fm(wv, wkey, nchunks, rhs_fn, rkeys, evac, nk=8, ncols=512):
            for j in range(nchunks):
                b = pa_ctr[0] % 4
                pa_ctr[0] += 1
                for kc in range(nk):
                    mm(PJ[b][:, 0:ncols], wv[:, kc, j * 128:(j + 1) * 128], rhs_fn(kc), kc == 0, kc == nk - 1,
                       [wkey] + [(rk_ % kc if "%d" in rk_ else rk_) for rk_ in rkeys], [PJK[b]])
                d_ = evac(j, PJ[b], PJK[b])
                if pend[0] is not None:
                    pend[0]()
                pend[0] = d_
            if pend[0] is not None:
                pend[0]()
                pend[0] = None

        pend = [None]
        cst_ctr = [0]

        def conv_a_chunk(ps, pkey, cidx):
            s = cst_ctr[0] % 2
            cst_ctr[0] += 1
            ck, ak = "cst%d" % s, "acc%d" % s
            wv = C["conv_a"]
            cp("pool", cst[:, s, 0:3], tails_a[:, cidx, :], ["tails_a%d" % cidx], [ck])
            cp("act", cst[:, s, 3:515], ps[:, :], [pkey], [ck])
            act(acc[:, s, :], cst[:, s, 3:515], AF.Identity, [ck, "c_conv_a"], [ak], scale=wv[:, cidx * 4 + 3:cidx * 4 + 4])
            for j in (2, 1, 0):
                stt(acc[:, s, :], cst[:, s, j:j + 512], wv[:, cidx * 4 + j:cidx * 4 + j + 1], acc[:, s, :],
                    ALU.mult, ALU.add, [ck, ak, "c_conv_a"], [ak])
            cp("pool", tails_a[:, cidx, :], cst[:, s, 512:515], [ck], ["tails_a%d" % cidx])
            return s, ak

        def l2n(s, ak, out_ap, okey, qscale):
            sk = "sqb%d" % s
            tt("pool", sqb[:, s, :], acc[:, s, :], acc[:, s, :], ALU.mult, [ak], [sk])
            mm(PZ[:, :], ones_bf[:, :], sqb[:, s, :], True, True, [sk, "ones_bf"], ["PZ"])
            act(rnb[:, :], PZ[:, :], AF.Ln, ["PZ"], ["rnb"], bias=1e-6)
            act(rnb[:, :], rnb[:, :], AF.Exp, ["rnb"], ["rnb"], scale=-0.5,
                bias=(-0.5 * math.log(128.0) if qscale else 0.0))
            tt("dve", out_ap, acc[:, s, :], rnb[:, :], ALU.mult, [ak, "rnb"], [okey])

        def gate_prep(wv, wkey):
            for pp in range(4):
                for kc in range(8):
                    mm(PZ[:, pp * 16:(pp + 1) * 16], hb[:, kc, pp * 128:(pp + 1) * 128], wv[:, kc, 0:16], kc == 0, kc == 7,
                       [wkey, "hb%d" % kc], ["PZ"])
            pz3 = PZ[:, 0:64].rearrange("p (c n) -> p c n", c=4)
            act(btm[:, :].rearrange("p (c n) -> p c n", c=4), pz3[:, :, 0:8], AF.Sigmoid, ["PZ"], ["btm"])
            tt("dve", tmt[:, :].rearrange("p (c n) -> p c n", c=4), pz3[:, :, 8:16],
               C["dtb_tm"][:, :].rearrange("p (c n) -> p c n", c=4), ALU.add, ["PZ", "c_dtb_tm"], ["tmt"])
            act(tmt[:, :], tmt[:, :], AF.Exp, ["tmt"], ["tmt"])
            act(tmt[:, :], tmt[:, :], AF.Ln, ["tmt"], ["tmt"], bias=1.0)
            stt(grtm[:, :], tmt[:, :], -1.0, nA_tm[:, :], ALU.mult, ALU.mult, ["tmt", "nA_tm"], ["grtm"])
            mm(PV[:, 0:32], C["tri_blk"][:, :], grtm[:, :], True, True, ["grtm", "c_tri_blk"], ["PV"])
            mm(PV[:, 32:64], C["sgt"][:, :], grtm[:, :], True, True, ["grtm", "c_sgt"], ["PV"])
            mm(PV[:, 64:96], C["onesh"][:, 0:128], grtm[:, :], True, True, ["grtm", "c_onesh"], ["PV"])
            mm(PV[:, 96:128], C["onesh"][:, 128:256], grtm[:, :], True, True, ["grtm", "c_onesh"], ["PV"])
            act(bge[:, :], PV[:, 0:32], AF.Exp, ["PV"], ["bge"])
            tt("dve", bge[:, :], bge[:, :], btm[:, :], ALU.mult, ["bge", "btm"], ["bge"])
            act(kd[:, :], PV[:, 32:64], AF.Exp, ["PV"], ["kd"])
            act(egl[:, :, :], PV[:, 64:128].rearrange("p (a b) -> p a b", a=2), AF.Exp, ["PV"], ["egl"])
            for kc in range(8):
                mm(PX[0:8, :], wv[:, kc, 0:8], hb[:, kc, :], kc == 0, kc == 7, [wkey, "hb%d" % kc], ["PX"])
            for kc in range(8):
                mm(PY[0:8, :], wv[:, kc, 8:16], hb[:, kc, :], kc == 0, kc == 7, [wkey, "hb%d" % kc], ["PY"])
            act(bfm[:, :], PX[0:8, :], AF.Sigmoid, ["PX"], ["bfm"])
            act(grfm[:, :], PY[0:8, :], AF.Exp, ["PY", "c_dtb_fm"], ["grfm"], bias=C["dtb_fm"][:, 0:1])
            act(grfm[:, :], grfm[:, :], AF.Ln, ["grfm"], ["grfm"], bias=1.0)
            ts("dve", grfm[:, :], grfm[:, :], nA_fm[:, 0:1], -1.0, ALU.mult, ALU.mult, ["grfm", "nA_fm"], ["grfm"])
            P.op("dve", lambda e: e.tensor_tensor_scan(out=gcf[:, :], data0=C["scanmask"][:, :], data1=grfm[:, :],
                                                       initial=0.0, op0=ALU.mult, op1=ALU.add),
                 ["grfm", "c_scanmask"], ["gcf"])
            act(gcf[:, :], gcf[:, :], AF.Exp, ["gcf"], ["gcf"])
            dbg("gcf", gcf[:, :], [8, 512], ["gcf"]); dbg("bfm", bfm[:, :], [8, 512], ["bfm"])

        sel3 = C["sel"][:, :].rearrange("p (h n) -> p h n", h=8)
        mk4 = C["masks4"][:, :].rearrange("p (l h n) -> p l h n", l=4, h=4)

        def gdn_k_stage(half, V):
            def f(wv, wkey):
                def evac(j, ps, pkey):
                    h = half * 4 + j
                    s, ak = conv_a_chunk(ps, pkey, 8 + h)

                    def later():
                        act(acc[:, s, :], acc[:, s, :], AF.Silu, [ak], [ak])
                        l2n(s, ak, V["kf"][:, j, :], V["k_kf"], False)
                        mm(PV[:, :], sel3[:, h, :], bfm[:, :], True, True, ["bfm", "c_sel"], ["PV"])
                        tt("dve", V["kq"][:, j, 0, :], V["kf"][:, j, :], PV[:, :], ALU.mult, [V["k_kf"], "PV"], [V["k_kq"]])
                    return later
                proj_fm(wv, wkey, 4, lambda kc: hb[:, kc, :], ["hb%d"], evac)
            return f

        def gdn_v_stage(half, V):
            def f(wv, wkey):
                def evac(j, ps, pkey):
                    h = half * 4 + j
                    s, ak = conv_a_chunk(ps, pkey, 16 + h)

                    def later():
                        act(V["vf"][:, j, :], acc[:, s, :], AF.Silu, [ak], [V["k_vf"]])
                    return later
                proj_fm(wv, wkey, 4, lambda kc: hb[:, kc, :], ["hb%d"], evac)
            return f

        def gdn_q_stage(half, V):
            def f(wv, wkey):
                def evac(j, ps, pkey):
                    h = half * 4 + j
                    s, ak = conv_a_chunk(ps, pkey, h)

                    def later():
                        act(acc[:, s, :], acc[:, s, :], AF.Silu, [ak], [ak])
                        l2n(s, ak, V["kq"][:, j, 1, :], V["k_kq"], True)
                        mm(PV[:, :], sel3[:, h, :], gcf[:, :], True, True, ["gcf", "c_sel"], ["PV"])
                        tt("dve", V["qe"][:, j, :], V["kq"][:, j, 1, :], PV[:, :], ALU.mult, [V["k_kq"], "PV"], [V["k_qe"]])
                    return later
                proj_fm(wv, wkey, 4, lambda kc: hb[:, kc, :], ["hb%d"], evac)
            return f

        def gdn_z_stage(half):
            def f(wv, wkey):
                def evac(j, ps, pkey):
                    act(zs[:, j, :], ps[:, :], AF.Silu, [pkey], ["zs"])
                proj_fm(wv, wkey, 4, lambda kc: hb[:, kc, :], ["hb%d"], evac)
            return f

        def PS(pr):
            return slice(pr * 64, pr * 64 + 64)

        BANKS_A = dict(g=[PG[0], PG[1]], gk=["PG0", "PG1"], m=[PX, PY], mk=["PX", "PY"])
        BANKS_B = dict(g=[PA[0], PA[1]], gk=["PA0", "PA1"], m=[PZ, PV], mk=["PZ", "PV"])

        def gdn_pair(half, pp, full, V, B, BK):
            T = B["tag"]
            kbg_, kdt_, vbt_, sgtg_, M2_, NQ_, Asb_ = B["kbg"], B["kdt"], B["vbt"], B["sgtg"], B["M2"], B["NQ"], B["Asb"]
            Lb_, YZ_, nM_, UM_, AM_, nwT_, vn_ = B["Lb"], B["YZ"], B["nM"], B["UM"], B["AM"], B["nwT"], B["vn"]
            kf_, kq_, vf_, qe_, oraw_ = V["kf"], V["kq"], V["vf"], V["qe"], V["oraw"]
            kkf, kkq, kvf, kqe, kor = V["k_kf"], V["k_kq"], V["k_vf"], V["k_qe"], V["k_oraw"]
            M0, M1 = BK["m"]
            k0, k1 = BK["mk"]
            Gs, gks = BK["g"], BK["gk"]
            sfk, sbk = "Sf%d" % half, "Sb%d" % half

            def K(n):
                return n + T
            tok = slice(pp * 128, (pp + 1) * 128)
            for hi in range(4):
                mm(M0[:, hi * 128:(hi + 1) * 128], kf_[:, hi, tok], ident_bf[:, :], True, True, [kkf, "ident_bf"], [k0])
                mm(M1[:, hi * 128:(hi + 1) * 128], vf_[:, hi, tok], ident_bf[:, :], True, True, [kvf, "ident_bf"], [k1])
            for hi in range(4):
                col = pp * 8 + half * 4 + hi
                act(kbg_[:, hi, :], M0[:, hi * 128:(hi + 1) * 128], AF.Identity, [k0, "bge"], [K("kbg")],
                    scale=bge[:, col:col + 1])
                ts("dve", kdt_[:, hi, :], M0[:, hi * 128:(hi + 1) * 128], kd[:, col:col + 1], None, ALU.mult, None,
                   [k0, "kd"], [K("kdt")])
                ts("dve", vbt_[:, hi, :], M1[:, hi * 128:(hi + 1) * 128], btm[:, col:col + 1], None, ALU.mult, None,
                   [k1, "btm"], [K("vbt")])
                ts("dve", sgtg_[:, hi, :], C["sgt"][:, :], grtm[:, col:col + 1], None, ALU.mult, None,
                   ["grtm", "c_sgt"], [K("sgtg")])
            yield
            for hi in range(4):
                for pr in range(2):
                    cs = slice(pp * 128 + pr * 64, pp * 128 + pr * 64 + 64)
                    mm(M0[PS(pr), hi * 128:(hi + 1) * 128], kf_[:, hi, cs], kq_[:, hi, :, cs], True, True, [kkf, kkq], [k0])
            for hi in range(4):
                mm(M1[:, hi * 128:(hi + 1) * 128], sgtg_[:, hi, :], C["tri2"][:, :], True, False, [K("sgtg"), "c_tri2"], [k1])
                mm(M1[:, hi * 128:(hi + 1) * 128], ident_bf[:, :], negm2_bf[:, :], False, True,
                   ["ident_bf", "negm2_bf"], [k1])
            act(M2_[:, :], M1[:, :], AF.Exp, [k1], [K("M2")])
            tt("dve", NQ_[:, :, :], M0[:, :].rearrange("p (h n) -> p h n", h=4),
               M2_[:, :].rearrange("p (h n) -> p h n", h=4), ALU.mult, [k0, K("M2")], [K("NQ")])
            yield
            for hi in range(4):
                for pr in range(2):
                    mm(M0[PS(pr), hi * 64:(hi + 1) * 64], NQ_[PS(pr), hi, 0:64], ident_bf[PS(pr), PS(pr)], True, True,
                       [K("NQ"), "ident_bf"], [k0])
            cp("act", Asb_[:, :, :], M0[:, 0:256].rearrange("p (h n) -> p h n", h=4), [k0], [K("Asb")])
            for l in range(4):
                tt("pool", UM_[:, l, :, :], NQ_[:, :, 0:64], mk4[:, l, :, :], ALU.mult, [K("NQ"), "c_masks4"], [K("UM")])
                tt("dve", AM_[:, l, :, :], Asb_[:, :, :], mk4[:, l, :, :], ALU.mult, [K("Asb"), "c_masks4"], [K("AM")])
            yield

            def idb(pr):
                return ident_bf[PS(pr), PS(pr)]

            def nidb(pr):
                return nident_bf[PS(pr), PS(pr)]
            for sub in range(2):
                pg, pk, lk = Gs[sub], gks[sub], K("L%d" % sub)
                for hh in range(2):
                    hi = sub * 2 + hh
                    o = hh * 256
                    for pr in range(2):
                        q_ = PS(pr)
                        U8, A8 = UM_[q_, 0, hi, :], AM_[q_, 0, hi, :]
                        mm(pg[q_, o:o + 64], A8, U8, True, True, [K("UM"), K("AM")], [pk])
                        mm(pg[q_, o + 64:o + 128], idb(pr), idb(pr), True, False, ["ident_bf"], [pk])
                        mm(pg[q_, o + 64:o + 128], nidb(pr), U8, False, True, ["nident_bf", K("UM")], [pk])
                        mm(pg[q_, o + 128:o + 192], U8, A8, True, True, [K("UM"), K("AM")], [pk])
                        mm(pg[q_, o + 192:o + 256], idb(pr), idb(pr), True, False, ["ident_bf"], [pk])
                        mm(pg[q_, o + 192:o + 256], nidb(pr), A8, False, True, ["nident_bf", K("AM")], [pk])
                cp(evac_eng(), Lb_[:, sub * 2:sub * 2 + 2, :], pg[:, :].rearrange("p (h n) -> p h n", h=2), [pk], [lk])
            yield
            for sub in range(2):
                pg, pk, lk = Gs[sub], gks[sub], K("L%d" % sub)
                for hh in range(2):
                    hi = sub * 2 + hh
                    o = hh * 256
                    for pr in range(2):
                        q_ = PS(pr)
                        X, Pm, XT, PTm = Lb_[q_, hi, 0:64], Lb_[q_, hi, 64:128], Lb_[q_, hi, 128:192], Lb_[q_, hi, 192:256]
                        mm(pg[q_, o:o + 128], XT, Lb_[q_, hi, 0:128], True, False, [lk], [pk])
                        mm(pg[q_, o + 64:o + 128], idb(pr), Pm, False, True, [lk, "ident_bf"], [pk])
                        mm(pg[q_, o + 128:o + 256], X, Lb_[q_, hi, 128:256], True, False, [lk], [pk])
                        mm(pg[q_, o + 192:o + 256], idb(pr), PTm, False, True, [lk, "ident_bf"], [pk])
                cp(evac_eng(), Lb_[:, sub * 2:sub * 2 + 2, :], pg[:, :].rearrange("p (h n) -> p h n", h=2), [pk], [lk])
            yield
            for sub in range(2):
                pg, pk, lk, yk = Gs[sub], gks[sub], K("L%d" % sub), K("YZ%d" % sub)
                for hh in range(2):
                    hi = sub * 2 + hh
                    o = hh * 256
                    for pr in range(2):
                        q_ = PS(pr)
                        X, Pm, XT, PTm = Lb_[q_, hi, 0:64], Lb_[q_, hi, 64:128], Lb_[q_, hi, 128:192], Lb_[q_, hi, 192:256]
                        mm(pg[q_, o + 64:o + 128], XT, Pm, True, False, [lk], [pk])
                        mm(pg[q_, o + 64:o + 128], idb(pr), Pm, False, True, [lk, "ident_bf"], [pk])
                        mm(pg[q_, o + 192:o + 256], X, PTm, True, False, [lk], [pk])
                        mm(pg[q_, o + 192:o + 256], idb(pr), PTm, False, True, [lk, "ident_bf"], [pk])
                pv = pg[:, :].rearrange("p (h n) -> p h n", h=2)
                cp("act", YZ_[:, sub * 2:sub * 2 + 2, 0:64], pv[:, :, 64:128], [pk], [yk])
                cp("dve", YZ_[:, sub * 2:sub * 2 + 2, 64:128], pv[:, :, 192:256], [pk], [yk])
            yield
            for l in range(1, 4):
                for sub in range(2):
                    pg, pk, yk, nk_ = Gs[sub], gks[sub], K("YZ%d" % sub), K("nM%d" % sub)
                    pv = pg[:, :].rearrange("p (h n) -> p h n", h=2)
                    for hh in range(2):
                        hi = sub * 2 + hh
                        o = hh * 256
                        for pr in range(2):
                            q_ = PS(pr)
                            mm(pg[q_, o:o + 64], AM_[q_, l, hi, :], YZ_[q_, hi, 0:64], True, True, [K("AM"), yk], [pk])
                            mm(pg[q_, o + 64:o + 128], UM_[q_, l, hi, :], YZ_[q_, hi, 64:128], True, True, [K("UM"), yk], [pk])
                    act(nM_[:, sub * 2:sub * 2 + 2, :], pv[:, :, 0:128], AF.Identity, [pk], [nk_], scale=-1.0)
                yield
                for sub in range(2):
                    pg, pk, yk, nk_ = Gs[sub], gks[sub], K("YZ%d" % sub), K("nM%d" % sub)
                    pv = pg[:, :].rearrange("p (h n) -> p h n", h=2)
                    for hh in range(2):
                        hi = sub * 2 + hh
                        o = hh * 256
                        for pr in range(2):
                            q_ = PS(pr)
                            mm(pg[q_, o + 128:o + 192], idb(pr), YZ_[q_, hi, 0:64], True, False, [yk, "ident_bf"], [pk])
                            mm(pg[q_, o + 128:o + 192], YZ_[q_, hi, 64:128], nM_[q_, hi, 0:64], False, True, [yk, nk_], [pk])
                            mm(pg[q_, o + 192:o + 256], idb(pr), YZ_[q_, hi, 64:128], True, False, [yk, "ident_bf"], [pk])
                            mm(pg[q_, o + 192:o + 256], YZ_[q_, hi, 0:64], nM_[q_, hi, 64:128], False, True, [yk, nk_], [pk])
                    cp("dve", YZ_[:, sub * 2:sub * 2 + 2, :], pv[:, :, 128:256], [pk], [yk])
                yield
            yzk = [K("YZ0"), K("YZ1")]
            for pr in range(2):
                q_ = PS(pr)
                c = 2 * pp + pr
                cs = slice(c * 64, (c + 1) * 64)
                for hi in range(4):
                    mm(M0[:, hi * 64:(hi + 1) * 64], kbg_[q_, hi, :], YZ_[q_, hi, 0:64], True, True, [K("kbg")] + yzk, [k0])
                act(nwT_[:, :, :], M0[:, 0:256].rearrange("p (h n) -> p h n", h=4), AF.Identity, [k0], [K("nwT")], scale=-1.0)
                yield
                for hi in range(4):
                    h = half * 4 + hi
                    mm(M1[q_, hi * 128:(hi + 1) * 128], YZ_[q_, hi, 0:64], vbt_[q_, hi, :], True, False, yzk + [K("vbt")], [k1])
                    mm(M1[q_, hi * 128:(hi + 1) * 128], nwT_[:, hi, :], Sb[:, h, :], False, True, [K("nwT"), sbk], [k1])
                cp("dve", vn_[q_, :, :], M1[q_, :].rearrange("p (h n) -> p h n", h=4), [k1], [K("vn")])
                for hi in range(4):
                    h = half * 4 + hi
                    col = pp * 8 + h
                    act(Sf[:, h, :], Sf[:, h, :], AF.Identity, [sfk, "egl"], [sfk], scale=egl[:, pr, col:col + 1])
                yield
                if full:
                    for hi in range(4):
                        h = half * 4 + hi
                        mm(M0[:, hi * 64:(hi + 1) * 64], Sb[:, h, :], qe_[:, hi, cs], True, False, [sbk, kqe], [k0])
                        mm(M0[:, hi * 64:(hi + 1) * 64], vn_[q_, hi, :], NQ_[q_, hi, 64:128], False, True, [K("vn"), K("NQ")], [k0])
                    cp("act", oraw_[:, :, cs], M0[:, 0:256].rearrange("p (h n) -> p h n", h=4), [k0], [kor])
                for hi in range(4):
                    mm(M1[:, hi * 128:(hi + 1) * 128], kdt_[q_, hi, :], vn_[q_, hi, :], True, True, [K("kdt"), K("vn")], [k1])
                hs = slice(half * 4, half * 4 + 4)
                m1v = M1[:, :].rearrange("p (h n) -> p h n", h=4)
                tt("dve", Sb[:, hs, :], Sf[:, hs, :], m1v, ALU.add, [sfk, k1], [sbk])
                tt("dve", Sf[:, hs, :], Sf[:, hs, :], m1v, ALU.add, [sfk, k1], [sfk])
                yield

        def interleave(gens):
            gens = list(gens)
            while gens:
                for g_ in list(gens):
                    try:
                        next(g_)
                    except StopIteration:
                        gens.remove(g_)

        def gdn_chain(half, full, V, B, BK):
            for pp in range(4):
                yield from gdn_pair(half, pp, full, V, B, BK)

        def gdn_finalize(half):
            dbg("oraw%d" % half, oraw[:, :, :], [128, 4, 512], ["oraw"])
            dbg("kf%d" % half, kf[:, :, :], [128, 4, 512], ["kf"])
            dbg("kq%d" % half, kq[:, :, :, :], [128, 4, 2, 512], ["kq"])
            dbg("vf%d" % half, vf[:, :, :], [128, 4, 512], ["vf"])
            dbg("qe%d" % half, qe[:, :, :], [128, 4, 512], ["qe"])
            for hi in range(4):
                h = half * 4 + hi
                s = hi % 2
                sk = "sqb%d" % s
                act(sqb[:, s, :], oraw[:, hi, :], AF.Square, ["oraw"], [sk])
                mm(PZ[:, :], ones_bf[:, :], sqb[:, s, :], True, True, [sk, "ones_bf"], ["PZ"])
                act(rnb[:, :], PZ[:, :], AF.Ln, ["PZ"], ["rnb"], bias=1e-6, scale=1.0 / 128.0)
                act(rnb[:, :], rnb[:, :], AF.Exp, ["rnb"], ["rnb"], scale=-0.5)
                stt(rn2[:, :], oraw[:, hi, :], C["norm_a"][:, 0:1], rnb[:, :], ALU.mult, ALU.mult,
                    ["oraw", "rnb", "c_norm_a"], ["rn2"])
                tt("dve", o_a[:, h, :], rn2[:, :], zs[:, hi, :], ALU.mult, ["rn2", "zs"], ["o_a"])

        def kb_stage(g, cur):
            def f(wv, wkey):
                def evac(j, ps, pkey):
                    cp(evac_eng(), Kbuf[:, g * 4 + j, cur, :], ps[:, :], [pkey], ["Kbuf%d" % cur])
                proj_fm(wv, wkey, 4, lambda kc: hb[:, kc, :], ["hb%d"], evac)
            return f

        def vb_stage(g, cur):
            def f(wv, wkey):
                for blk in range(4):
                    b = pa_ctr[0] % 4
                    pa_ctr[0] += 1
                    for kc in range(8):
                        mm(PJ[b][:, :], hb[:, kc, blk * 128:(blk + 1) * 128], wv[:, kc, :], kc == 0, kc == 7,
                           [wkey, "hb%d" % kc], [PJK[b]])
                    cp(evac_eng(), Vbuf[:, cur, blk, g * 512:(g + 1) * 512], PJ[b][:, :], [PJK[b]], ["Vbuf%d" % cur])
            return f

        def qb_stage(g):
            def f(wv, wkey):
                def evac(j, ps, pkey):
                    act(qb[:, g * 4 + j, :], ps[:, :], AF.Identity, [pkey], ["qb"], scale=0.125)
                proj_fm(wv, wkey, 4, lambda kc: hb[:, kc, :], ["hb%d"], evac)
            return f

        sc_ctr = [0]

        def attention_gen(cur, use_cmask, pairs):
            prev = 1 - cur
            blocks = []
            off = 0
            for b in range(8):
                c_lo, c_hi = max(0, 2 * b - 8), min(7, 2 * b + 1)
                nq = (c_hi - c_lo + 1) * 64
                blocks.append((b, c_lo, c_hi, nq, off))
                off += nq
            for m in pairs:
                for hh in range(2):
                    h = 2 * m + hh
                    rows = slice(hh * 64, hh * 64 + 64)
                    for (b, c_lo, c_hi, nq, off) in blocks:
                        half = prev if b < 4 else cur
                        kcols = slice((b % 4) * 128, (b % 4) * 128 + 128)
                        sc = sc_ctr[0] % 2
                        sc_ctr[0] += 1
                        pg, pk = PA[sc], "PA%d" % sc
                        seq = []
                        seq.append((pg[:, 0:nq], Kbuf[rows, m, half, kcols], qb[rows, m, c_lo * 64:(c_hi + 1) * 64],
                                    ["Kbuf%d" % half, "qb"]))
                        cb_hi = min(c_hi, 2 * b - 3)
                        if cb_hi >= c_lo:
                            n2 = (cb_hi - c_lo + 1) * 64
                            d0 = (c_lo - 2 * b + 8) * 64
                            seq.append((pg[:, 0:n2], ident_bf[:, :], biasbf[:, h, d0:d0 + n2], ["biasbf", "ident_bf"]))
                        if 2 * b + 1 <= 7:
                            o1 = (2 * b + 1 - c_lo) * 64
                            seq.append((pg[:, o1:o1 + 64], ident_bf[:, :], maskd1_bf[:, :], ["maskd1_bf", "ident_bf"]))
                        if use_cmask and b < 4:
                            seq.append((pg[:, 0:nq], ident_bf[:, :], cmask_bf[:, 0:nq], ["cmask_bf", "ident_bf"]))
                        for i, (o_, l_, r_, ks) in enumerate(seq):
                            mm(o_, l_, r_, i == 0, i == len(seq) - 1, ks, [pk])
                        act(PTb[:, off:off + nq], pg[:, 0:nq], AF.Exp, [pk], ["PTb"])
                        yield
                    order_b = [3, 0, 1, 2, 4, 5, 6, 7]
                    for i, b in enumerate(order_b):
                        (_, c_lo, c_hi, nq, off) = blocks[b]
                        half = prev if b < 4 else cur
                        qs = slice(c_lo * 64, (c_hi + 1) * 64)
                        mm(PZ[rows, qs], Vbuf[:, half, b % 4, h * 64:(h + 1) * 64], PTb[:, off:off + nq], i == 0, i == 7,
                           ["Vbuf%d" % half, "PTb"], ["PZ"])
                    yield
                    for i, b in enumerate(order_b):
                        (_, c_lo, c_hi, nq, off) = blocks[b]
                        qs = slice(c_lo * 64, (c_hi + 1) * 64)
                        mm(PV[rows, qs], ones_bf[:, 0:64], PTb[:, off:off + nq], i == 0, i == 7, ["ones_bf", "PTb"], ["PV"])
                    yield
                act(rcb[:, :], PV[:, :], AF.Ln, ["PV"], ["rcb"])
                act(rcb[:, :], rcb[:, :], AF.Exp, ["rcb"], ["rcb"], scale=-1.0)
                tt("dve", o_b[:, m, :], PZ[:, :], rcb[:, :], ALU.mult, ["PZ", "rcb"], ["o_b"])
                yield

        FULLC = slice(0, 512)

        def gate_stage(g, cs=FULLC):
            n = cs.stop - cs.start

            def f(wv, wkey):
                def evac(j, ps, pkey):
                    jj = g * 4 + j
                    act(gsig[:, jj, cs], ps[:, 0:n], AF.Sigmoid, [pkey, "c_b_gate"], ["gsig"], bias=C["b_gate"][:, jj:jj + 1])
                proj_fm(wv, wkey, 4, lambda kc: hb[:, kc, cs], ["hb%d"], evac, ncols=n)
            return f

        def wa_stage(g, cs=FULLC):
            n = cs.stop - cs.start

            def f(wv, wkey):
                def evac(j, ps, pkey):
                    tt("dve", ta_[:, j, cs], ps[:, 0:n], gsig[:, g * 4 + j, cs], ALU.mult, [pkey, "gsig"], ["ta"])
                proj_fm(wv, wkey, 4, lambda kc: o_a[:, kc, cs], ["o_a"], evac, ncols=n)
            return f

        def wb_stage(g, cs=FULLC):
            n = cs.stop - cs.start

            def f(wv, wkey):
                def evac(j, ps, pkey):
                    s = j % 2
                    tt("dve", acc[:, s, 0:n], ps[:, 0:n], gsig[:, 8 + g * 4 + j, cs], ALU.mult, [pkey, "gsig"], ["acc%d" % s])
                    tt("pool", merged[:, g * 4 + j, cs], acc[:, s, 0:n], ta_[:, j, cs], ALU.add, ["acc%d" % s, "ta"], ["merged"])
                proj_fm(wv, wkey, 4, lambda kc: o_b[:, kc, cs], ["o_b"], evac, ncols=n)
            return f

        def wo_stage(g, cs=FULLC):
            n = cs.stop - cs.start

            def f(wv, wkey):
                def evac(j, ps, pkey):
                    jj = g * 4 + j
                    stt(xa[:, jj, cs], ps[:, 0:n], g_t[:, jj:jj + 1], xa[:, jj, cs], ALU.mult, ALU.add,
                        [pkey, "xa%d" % jj, "modv"], ["xa%d" % jj])
                proj_fm(wv, wkey, 4, lambda kc: merged[:, kc, cs], ["merged"], evac, ncols=n)
            return f

        def layernorm(y, ykey, outs, cs=FULLC):
            n = cs.stop - cs.start

            def yk(kc):
                return (ykey % kc) if "%d" in ykey else ykey
            for kc in range(8):
                s = kc % 2
                cp("dve", ybt[:, s, 0:n], y[:, kc, cs], [yk(kc)], ["ybt%d" % s])
                act(sqb[:, s, 0:n], y[:, kc, cs], AF.Square, [yk(kc)], ["sqb%d" % s])
                mm(PX[:, 0:n], onesD_bf[:, :], ybt[:, s, 0:n], kc == 0, kc == 7, ["ybt%d" % s, "onesD_bf"], ["PX"])
                mm(PY[:, 0:n], onesD_bf[:, :], sqb[:, s, 0:n], kc == 0, kc == 7, ["sqb%d" % s, "onesD_bf"], ["PY"])
            act(rn2[:, 0:n], PX[:, 0:n], AF.Square, ["PX"], ["rn2"])
            tt("dve", rn2[:, 0:n], PY[:, 0:n], rn2[:, 0:n], ALU.subtract, ["PY", "rn2"], ["rn2"])
            act(rn2[:, 0:n], rn2[:, 0:n], AF.Ln, ["rn2"], ["rn2"], bias=1e-5)
            act(rn2[:, 0:n], rn2[:, 0:n], AF.Exp, ["rn2"], ["rn2"], scale=-0.5)
            cp("dve", rnb[:, 0:n], PX[:, 0:n], ["PX"], ["rnb"])
            pnd = None
            for kc in range(8):
                s = kc % 2
                ak = "acc%d" % s
                tt("dve", acc[:, s, 0:n], y[:, kc, cs], rnb[:, 0:n], ALU.subtract, [yk(kc), "rnb"], [ak])
                tt("dve", acc[:, s, 0:n], acc[:, s, 0:n], rn2[:, 0:n], ALU.mult, [ak, "rn2"], [ak])

                def later(kc=kc, s=s, ak=ak):
                    for (ofn, okey, sc, bi, rk, eng) in outs:
                        okey = (okey % kc) if "%d" in okey else okey
                        if eng == "alt":
                            eng = "act" if kc % 2 == 0 else "dve"
                        if eng == "act":
                            act(ofn(kc), acc[:, s, 0:n], AF.Identity, [ak] + rk, [okey], bias=bi[:, kc:kc + 1], scale=sc[:, kc:kc + 1])
                        else:
                            ts(eng, ofn(kc), acc[:, s, 0:n], sc[:, kc:kc + 1], bi[:, kc:kc + 1], ALU.mult, ALU.add, [ak] + rk, [okey])
                if pnd is not None:
                    pnd()
                pnd = later
            pnd()

        def up_stage(g):
            def f(wv, wkey):
                def evac(j, ps, pkey):
                    m = g * 4 + j
                    s = cst_ctr[0] % 2
                    cst_ctr[0] += 1
                    ck, ak = "cst%d" % s, "acc%d" % s
                    wv_ = C["conv_f"]
                    cp("pool", cst[:, s, 0:2], tails_f[:, m, :], ["tails_f%d" % m], [ck])
                    cp("act", cst[:, s, 2:514], ps[:, :], [pkey], [ck])
                    act(acc[:, s, :], cst[:, s, 2:514], AF.Identity, [ck, "c_conv_f", "c_bconv"], [ak],
                        scale=wv_[:, m * 3 + 2:m * 3 + 3], bias=C["bconv"][:, m:m + 1])
                    for jt in (1, 0):
                        stt(acc[:, s, :], cst[:, s, jt:jt + 512], wv_[:, m * 3 + jt:m * 3 + jt + 1], acc[:, s, :],
                            ALU.mult, ALU.add, [ck, ak, "c_conv_f"], [ak])
                    cp("pool", tails_f[:, m, :], cst[:, s, 512:514], [ck], ["tails_f%d" % m])

                    def later():
                        if m < 22:
                            act(sg[:, m, :], acc[:, s, :], AF.Silu, [ak], ["sg%d" % m])
                        else:
                            tt("pool", sg[:, m - 22, :], acc[:, s, :], sg[:, m - 22, :], ALU.mult, [ak, "sg%d" % (m - 22)],
                               ["sg%d" % (m - 22)])
                    return later
                proj_fm(wv, wkey, 4, lambda kc: hb[:, kc, :], ["hb%d"], evac)
            return f

        def up_stage_halo(g):
            def f(wv, wkey):
                def evac(j, ps, pkey):
                    m = g * 4 + j
                    cp("act", tails_f[:, m, :], ps[:, 62:64], [pkey], ["tails_f%d" % m])
                proj_fm(wv, wkey, 4, lambda kc: hb[:, kc, 448:512], ["hb%d"], evac, ncols=64)
            return f

        def dn_stage(j):
            def f(wv, wkey):
                b = pa_ctr[0] % 4
                pa_ctr[0] += 1
                for m in range(22):
                    mm(PJ[b][:, :], wv[:, m, :], sg[:, m, :], m == 0, m == 21, [wkey, "sg%d" % m], [PJK[b]])
                stt(x1[:, j, :], PJ[b][:, :], g_f[:, j:j + 1], x1[:, j, :], ALU.mult, ALU.add,
                    [PJK[b], "x1", "modv"], ["x1"])
            return f

        GDN_KEYS = ["kf", "kq", "qe", "vf", "oraw", "zs"]
        ATT_KEYS = ["qb", "PTb"]
        MRG_KEYS = ["gsig", "ta", "merged"]
        SG_KEYS = ["sg%d" % m for m in range(22)]

        stages = []

        def add(gname, fn):
            stages.append((gname, fn))

        premade = {}
        deferred_ln2 = [None]
        for t, mode in enumerate(modes):
            full = (mode == "F")
            cur = t % 2

            def s_begin(wv=None, wkey=None, t=t, mode=mode):
                if t % 2 == 0:
                    P.next_epoch()
                if t == 0:
                    load_x(0)
                if t == first_main:
                    for hh_ in range(8):
                        pass
                    ts("dve", Sf[:, :, :], Sf[:, :, :], C["flag"][:, 0:1], None, ALU.mult, None, ["Sf0", "Sf1", "c_flag"], ["Sf0", "Sf1"])
                    ts("dve", Sb[:, :, :], Sb[:, :, :], C["flag"][:, 0:1], None, ALU.mult, None, ["Sb0", "Sb1", "c_flag"], ["Sb0", "Sb1"])
                    ts("dve", tails_a[:, :, :], tails_a[:, :, :], C["flag"][:, 0:1], None, ALU.mult, None,
                       ["tails_a%d" % i for i in range(24)] + ["c_flag"], ["tails_a%d" % i for i in range(24)])
                    ts("dve", tails_f[:, :, :], tails_f[:, :, :], C["flag"][:, 0:1], None, ALU.mult, None,
                       ["tails_f%d" % i for i in range(44)] + ["c_flag"], ["tails_f%d" % i for i in range(44)])
                if mode == "F":
                    barrier(SG_KEYS + MRG_KEYS + SMODE_KEYS, GDN_KEYS)
                    if t > 0 and modes[t - 1] != "F":
                        barrier(SETB_KEYS, ["o_a", "o_b", "rn2"])
                else:
                    barrier(SG_KEYS + MRG_KEYS + GDN_KEYS, SMODE_KEYS)
                if not premade.get(t):
                    make_h(mode == "F")
                if mode != "F" and t + 1 < NT:
                    load_x(t + 1)
            add(None, s_begin)
            add("ba", gate_prep)
            if full:
                for g in range(2):
                    add("kb%d" % g, kb_stage(g, cur))
                for g in range(2):
                    add("vb%d" % g, vb_stage(g, cur))
                if deferred_ln2[0] is not None:
                    add(None, deferred_ln2[0])
                    deferred_ln2[0] = None
                for half in range(2):
                    add("ka%d" % half, gdn_k_stage(half, VW["F"]))
                    add("va%d" % half, gdn_v_stage(half, VW["F"]))
                    add("qa%d" % half, gdn_q_stage(half, VW["F"]))
                    if half == 0:
                        add(None, lambda wv=None, wkey=None: barrier(["x1", "ybt0", "ybt1"], ["qb", "PTb", "rcb"]))
                        for g in range(2):
                            add("qb%d" % g, qb_stage(g))

                    def s_gdn(wv=None, wkey=None, half=half, cur=cur, t=t):
                        interleave([gdn_chain(half, True, VW["F"], setA, BANKS_A),
                                    attention_gen(cur, t == first_main, range(half * 4, half * 4 + 4))])
                    add(None, s_gdn)
                    add("za%d" % half, gdn_z_stage(half))
                    add(None, lambda wv=None, wkey=None, half=half: gdn_finalize(half))
            else:
                for half in range(2):
                    add("ka%d" % half, gdn_k_stage(half, VW["S%d" % half]))
                    add("va%d" % half, gdn_v_stage(half, VW["S%d" % half]))

                def s_gdn2(wv=None, wkey=None):
                    interleave([gdn_chain(0, False, VW["S0"], setA, BANKS_A),
                                gdn_chain(1, False, VW["S1"], setB, BANKS_B)])
                add(None, s_gdn2)
            if mode == "SKV":
                for g in range(2):
                    add("kb%d" % g, kb_stage(g, cur))
                for g in range(2):
                    add("vb%d" % g, vb_stage(g, cur))
            if full:
                add(None, lambda wv=None, wkey=None: (dbg("o_a", o_a[:, :, :], [128, 8, 512], ["o_a"]), dbg("o_b", o_b[:, :, :], [128, 8, 512], ["o_b"])))
                add(None, lambda wv=None, wkey=None: (barrier(GDN_KEYS, MRG_KEYS), barrier(["qb", "PTb", "rcb"], ["x1", "ybt0", "ybt1"])))
                halo = (t == first_main - 1)
                hcs = slice(448, 512) if halo else FULLC
                for g in range(4):
                    add("g%d" % g, gate_stage(g, hcs))
                for g in range(2):
                    add("wa%d" % g, wa_stage(g, hcs))
                    add("wb%d" % g, wb_stage(g, hcs))
                for g in range(2):
                    add("wo%d" % g, wo_stage(g, hcs))

                def s_ln1(wv=None, wkey=None, t=t, hcs=hcs, halo=halo):
                    if halo:
                        layernorm(xa, "xa%d", [(lambda kc: hb[:, kc, hcs], "hb%d", G1h, B1h, ["G1h", "B1h"], "alt")], hcs)
                    else:
                        layernorm(xa, "xa%d", [
                            (lambda kc: x1[:, kc, :], "x1", G1a, B1a, ["G1a", "B1a"], "act"),
                            (lambda kc: hb[:, kc, :], "hb%d", G1h, B1h, ["G1h", "B1h"], "alt"),
                        ])
                    if t + 1 < NT:
                        load_x(t + 1)
                    barrier(MRG_KEYS, SG_KEYS)
                add(None, s_ln1)
                halo = (t == first_main - 1)
                for g in range(11):
                    add("up%d" % g, up_stage_halo(g) if halo else up_stage(g))
                if t + 1 < NT and modes[t + 1] == "F":
                    premade[t + 1] = True
                    add(None, lambda wv=None, wkey=None: make_h(True))
                if not halo:
                    for j in range(8):
                        add("dn%d" % j, dn_stage(j))

                def s_ln2(wv=None, wkey=None, t=t):
                    layernorm(x1, "x1", [
                        (lambda kc: x1[:, kc, :], "x1", lnp[:, 16:24], lnp[:, 24:32], ["c_lnp"], "act"),
                    ])
                    if t >= out_start:
                        to = t - out_start
                        dst = outT.rearrange("(k p) n -> p k n", p=128)[:, :, to * W:(to + 1) * W]
                        P.dma("sp", [(dst, x1[:, :, :])], reads=["x1"], writes=["out"], sem="st_out")
                if not halo:
                    deferred_ln2[0] = s_ln2
        if deferred_ln2[0] is not None:
            add(None, deferred_ln2[0])

        widx = [i for i, (g, f) in enumerate(stages) if g is not None]
        loaded = {}
        nxt = [0]

        def ensure_loaded(upto):
            while nxt[0] < len(widx) and nxt[0] <= upto:
                i = widx[nxt[0]]
                loaded[i] = wload(stages[i][0])
                nxt[0] += 1

        wpos = {i: k for k, i in enumerate(widx)}
        ensure_loaded(NSLOT - 2)
        import os
        kstop = int(os.environ.get("KSTOP", "100000"))
        for i, (g, f) in enumerate(stages):
            if i >= kstop:
                break
            if g is None:
                f()
            else:
                k = wpos[i]
                ensure_loaded(k + NSLOT - 1)
                wv, wkey = loaded[i]
                f(wv, wkey)
        P.op("sp", None, reads=["out"] + ["dbg_" + n for n in dbg_out])
        P.emit()
    return nc, P, list(dbg_out.keys())


_CACHE = {}


def run_cores(inp, core_specs, modes, out_start, first_main, dbg_names=()):
    key = (tuple(modes), out_start, first_main, tuple(dbg_names))
    if key not in _CACHE:
        _CACHE[key] = build(modes, out_start, first_main, dbg_names)
    nc, P, dn = _CACHE[key]
    shared = host_shared(inp)
    x = np.asarray(inp["x"], np.float32)
    c = np.asarray(inp["c"], np.float32)
    NT = len(modes)
    in_maps = []
    for (b, tok0, flag) in core_specs:
        m = dict(shared)
        xt = np.zeros((1024, NT * W), np.float32)
        lo = tok0
        n = NT * W
        s0 = max(lo, 0)
        if lo + n > s0:
            xt[:, s0 - lo:] = x[b, s0:lo + n, :].T
        m["xT"] = xt
        m["ccol"] = fm_vec(c[b], 8)
        m["flag"] = np.full((128, 1), float(flag), np.float32)
        m["cmask"] = np.full((128, 512), 0.0 if flag else NEG, np.float32)
        in_maps.append(m)
    res = run_bass_kernel_spmd(nc, in_maps, core_ids=list(range(len(core_specs))))
    return res.results


def kernel(**inputs):
    x = np.asarray(inputs["x"])
    B, S, Dm = x.shape
    half = S // 2
    n_main = half // W
    modes = ["S"] * (n_main - 2) + ["SKV", "F"] + ["F"] * n_main
    out_start = n_main
    first_main = n_main
    specs = []
    for b in range(B):
        specs.append((b, -half, 0))
        specs.append((b, 0, 1))
    res = run_cores(inputs, specs, modes, out_start, first_main)
    out = np.empty((B, S, Dm), np.float32)
    for i, (b, tok0, flag) in enumerate(specs):
        o = res[i]["outT"]
        h = 1 if flag else 0
        out[b, h * half:(h + 1) * half, :] = o.T
    return out
```
